# Optimizing a Trainium2 kernel written in Bass

```python
import jax, jax.numpy as jnp
from jax import lax
import numpy as np

D_MODEL = 1024
BATCH = 2
SEQ = 16384
DEPTH = 4

RET_HEADS = 4
RET_QK_DIM = 128
RET_V_DIM = 256
MLSTM_HEADS = 4
MLSTM_QK_DIM = 128
MLSTM_V_DIM = 256
CONV_WIDTH = 4
CHUNK = 128
D_FF = -(-8 * D_MODEL // (3 * 256)) * 256
ROPE_BASE = 10000.0
NORM_EPS = 1e-5
DEEPNORM_ALPHA = (2 * DEPTH) ** 0.25
DEEPNORM_BETA = (8 * DEPTH) ** -0.25

RET_QK = RET_HEADS * RET_QK_DIM
RET_V = RET_HEADS * RET_V_DIM
ML_QK = MLSTM_HEADS * MLSTM_QK_DIM
ML_V = MLSTM_HEADS * MLSTM_V_DIM
IN_SIZES = (RET_QK, RET_QK, RET_V, RET_V, 2 * ML_QK, ML_V, ML_V, MLSTM_HEADS, MLSTM_HEADS, D_MODEL, D_MODEL)
IN_BETA = (1.0, 1.0, DEEPNORM_BETA, 1.0, 1.0, DEEPNORM_BETA, 1.0, 1.0, 1.0, 1.0, 1.0)
IN_OFFSETS = tuple(int(o) for o in np.cumsum(IN_SIZES)[:-1])
D_IN = int(sum(IN_SIZES))

kernel_name = "retnet_mlstm_gated_hybrid_deepnorm"


def _layer_norm(x, g, b):
    xf = x.astype(jnp.float32)
    mu = xf.mean(-1, keepdims=True)
    var = jnp.square(xf - mu).mean(-1, keepdims=True)
    return ((xf - mu) * lax.rsqrt(var + NORM_EPS) * g.astype(jnp.float32) + b.astype(jnp.float32)).astype(x.dtype)


def _head_norm(h, g):
    B, S, H, d = h.shape
    mu = h.mean(-1, keepdims=True)
    var = jnp.square(h - mu).mean(-1, keepdims=True)
    hn = ((h - mu) * lax.rsqrt(var + NORM_EPS)).reshape(B, S, H * d)
    return hn * g.astype(jnp.float32)


def _rotary(x, pos):
    half = x.shape[-1] // 2
    inv_freq = ROPE_BASE ** (-jnp.arange(half, dtype=jnp.float32) / half)
    ang = pos.astype(jnp.float32)[:, None] * inv_freq[None, :]
    cos = jnp.cos(ang)[None, :, None, :]
    sin = jnp.sin(ang)[None, :, None, :]
    x1, x2 = x[..., :half], x[..., half:]
    return jnp.concatenate([x1 * cos - x2 * sin, x1 * sin + x2 * cos], axis=-1)


def _causal_depthwise_conv(x, w, b):
    C = x.shape[-1]
    y = lax.conv_general_dilated(
        x, w[:, None, :].astype(x.dtype), window_strides=(1,), padding=[(CONV_WIDTH - 1, 0)],
        dimension_numbers=('NWC', 'WIO', 'NWC'), feature_group_count=C)
    return y + b.astype(x.dtype)


def _retention(q, k, v):
    B, S, H, dk = q.shape
    dv = v.shape[-1]
    L = CHUNK
    N = S // L
    log_gamma = jnp.log(1.0 - jnp.power(2.0, -5.0 - jnp.arange(H, dtype=jnp.float32)))
    q = q.reshape(B, N, L, H, dk)
    k = k.reshape(B, N, L, H, dk)
    v = v.reshape(B, N, L, H, dv)
    idx = jnp.arange(L, dtype=jnp.float32)
    rel = idx[:, None] - idx[None, :]
    causal = rel >= 0
    decay = jnp.where(causal[None], jnp.exp(log_gamma[:, None, None] * jnp.where(causal, rel, 0.0)[None]), 0.0)
    scores = jnp.einsum('bnihd,bnjhd->bnhij', q, k) * decay[None, None]
    o_intra = jnp.einsum('bnhij,bnjhe->bnihe', scores, v)
    k_dec = k * jnp.exp(log_gamma[None, :] * (L - 1 - idx)[:, None])[None, None, :, :, None]
    kv = jnp.einsum('bnjhd,bnjhe->nbhde', k_dec, v)
    chunk_decay = jnp.exp(log_gamma * L)[None, :, None, None]

    def step(R, kv_n):
        return R * chunk_decay + kv_n, R

    _, R_prev = lax.scan(step, jnp.zeros((B, H, dk, dv), jnp.float32), kv)
    q_dec = q * jnp.exp(log_gamma[None, :] * (idx + 1.0)[:, None])[None, None, :, :, None]
    o_inter = jnp.einsum('bnihd,nbhde->bnihe', q_dec, R_prev)
    return (o_intra + o_inter).reshape(B, S, H, dv)


def _mlstm(q, k, v, i_pre, f_pre):
    B, S, H, dk = q.shape
    dv = v.shape[-1]
    L = CHUNK
    N = S // L
    k = k * (dk ** -0.5)
    q = q.reshape(B, N, L, H, dk)
    k = k.reshape(B, N, L, H, dk)
    v = v.reshape(B, N, L, H, dv)
    log_f = jax.nn.log_sigmoid(f_pre).reshape(B, N, L, H).transpose(0, 1, 3, 2)
    log_i = i_pre.reshape(B, N, L, H).transpose(0, 1, 3, 2)
    b = jnp.cumsum(log_f, axis=-1)
    b_end = b[..., -1]
    causal = jnp.tril(jnp.ones((L, L), dtype=bool))
    log_D = jnp.where(causal, b[..., :, None] - b[..., None, :] + log_i[..., None, :], -jnp.inf)
    log_w_end = b_end[..., None] - b + log_i
    m_loc = log_w_end.max(-1)
    w_end = jnp.exp(log_w_end - m_loc[..., None])
    kv = jnp.einsum('bnhs,bnshd,bnshe->nbhde', w_end, k, v)
    ksum = jnp.einsum('bnhs,bnshd->nbhd', w_end, k)

    def step(carry, inp):
        C, n, m = carry
        kv_n, ks_n, mloc_n, bend_n = inp
        m_new = jnp.maximum(bend_n + m, mloc_n)
        a = jnp.exp(bend_n + m - m_new)
        c = jnp.exp(mloc_n - m_new)
        C_new = a[..., None, None] * C + c[..., None, None] * kv_n
        n_new = a[..., None] * n + c[..., None] * ks_n
        return (C_new, n_new, m_new), (C, n, m)

    init = (jnp.zeros((B, H, dk, dv), jnp.float32), jnp.zeros((B, H, dk), jnp.float32), jnp.zeros((B, H), jnp.float32))
    _, (C_prev, n_prev, m_prev) = lax.scan(
        step, init, (kv, ksum, m_loc.transpose(1, 0, 2), b_end.transpose(1, 0, 2)))
    log_inter = b + m_prev.transpose(1, 0, 2)[..., None]
    m_row = jnp.maximum(log_D.max(-1), log_inter)
    D = jnp.exp(log_D - m_row[..., None])
    inter = jnp.exp(log_inter - m_row)
    qk = jnp.einsum('bnthd,bnshd->bnhts', q, k) * D
    num = (jnp.einsum('bnhts,bnshe->bnthe', qk, v)
           + jnp.einsum('bnthd,nbhde->bnthe', q, C_prev) * inter.transpose(0, 1, 3, 2)[..., None])
    den = qk.sum(-1) + jnp.einsum('bnthd,nbhd->bnht', q, n_prev) * inter
    denom = jnp.maximum(jnp.abs(den), jnp.exp(-m_row)).transpose(0, 1, 3, 2)[..., None]
    return (num / denom).reshape(B, S, H, dv)


def setup_inputs(seed: int = 0) -> dict:
    key = jax.random.key(seed)
    ks = jax.random.split(key, 20)
    f32 = jnp.float32

    def nrm(k, shape, scale):
        return jax.random.normal(k, shape, f32) * scale

    x = nrm(ks[0], (BATCH, SEQ, D_MODEL), 1.0)
    col_scale = jnp.concatenate([jnp.full((n,), c, f32) for n, c in zip(IN_SIZES, IN_BETA)])
    w_in = nrm(ks[1], (DEPTH, D_MODEL, D_IN), D_MODEL ** -0.5) * col_scale
    b_if = jnp.concatenate([
        nrm(ks[2], (DEPTH, MLSTM_HEADS), 0.1),
        jnp.linspace(3.0, 6.0, MLSTM_HEADS, dtype=f32)[None, :] + nrm(ks[3], (DEPTH, MLSTM_HEADS), 0.1)], axis=-1)
    b_merge = nrm(ks[4], (DEPTH, 2 * D_MODEL), 0.02)
    conv_w = nrm(ks[5], (DEPTH, CONV_WIDTH, 2 * ML_QK), CONV_WIDTH ** -0.5)
    conv_b = nrm(ks[6], (DEPTH, 2 * ML_QK), 0.02)
    ret_norm_g = 1.0 + nrm(ks[7], (DEPTH, RET_V), 0.02)
    mlstm_norm_g = 1.0 + nrm(ks[8], (DEPTH, ML_V), 0.02)
    w_proj_ret = nrm(ks[9], (DEPTH, RET_V, D_MODEL), RET_V ** -0.5 * DEEPNORM_BETA)
    w_proj_mlstm = nrm(ks[10], (DEPTH, ML_V, D_MODEL), ML_V ** -0.5 * DEEPNORM_BETA)
    w_out = nrm(ks[11], (DEPTH, D_MODEL, D_MODEL), D_MODEL ** -0.5 * DEEPNORM_BETA)
    ln1_g = 1.0 + nrm(ks[12], (DEPTH, D_MODEL), 0.02)
    ln1_b = nrm(ks[13], (DEPTH, D_MODEL), 0.02)
    w_gate_up = nrm(ks[14], (DEPTH, D_MODEL, 2 * D_FF), D_MODEL ** -0.5)
    w_down = nrm(ks[15], (DEPTH, D_FF, D_MODEL), D_FF ** -0.5 * DEEPNORM_BETA)
    ln2_g = 1.0 + nrm(ks[16], (DEPTH, D_MODEL), 0.02)
    ln2_b = nrm(ks[17], (DEPTH, D_MODEL), 0.02)
    return {"x": x, "w_in": w_in, "b_if": b_if, "b_merge": b_merge, "conv_w": conv_w, "conv_b": conv_b,
            "ret_norm_g": ret_norm_g, "mlstm_norm_g": mlstm_norm_g, "w_proj_ret": w_proj_ret,
            "w_proj_mlstm": w_proj_mlstm, "w_out": w_out, "ln1_g": ln1_g, "ln1_b": ln1_b,
            "w_gate_up": w_gate_up, "w_down": w_down, "ln2_g": ln2_g, "ln2_b": ln2_b}


def reference(x, w_in, b_if, b_merge, conv_w, conv_b, ret_norm_g, mlstm_norm_g, w_proj_ret,
              w_proj_mlstm, w_out, ln1_g, ln1_b, w_gate_up, w_down, ln2_g, ln2_b):
    B, S, _ = x.shape
    dt = x.dtype
    f32 = jnp.float32
    pos = jnp.arange(S, dtype=jnp.int32)
    for l in range(DEPTH):
        proj = x @ w_in[l]
        (r_q, r_k, r_v, r_g, m_qk, m_v, m_o, m_i, m_f, g_a, g_b) = jnp.split(proj, IN_OFFSETS, axis=-1)

        rq = _rotary(r_q.astype(f32).reshape(B, S, RET_HEADS, RET_QK_DIM), pos)
        rk = _rotary(r_k.astype(f32).reshape(B, S, RET_HEADS, RET_QK_DIM), pos) * (RET_QK_DIM ** -0.5)
        rv = r_v.astype(f32).reshape(B, S, RET_HEADS, RET_V_DIM)
        o_ret = _head_norm(_retention(rq, rk, rv), ret_norm_g[l]) * jax.nn.silu(r_g.astype(f32))
        y_ret = o_ret.astype(dt) @ w_proj_ret[l]

        qk_c = jax.nn.silu(_causal_depthwise_conv(m_qk, conv_w[l], conv_b[l])).astype(f32)
        mq = qk_c[..., :ML_QK].reshape(B, S, MLSTM_HEADS, MLSTM_QK_DIM)
        mk = qk_c[..., ML_QK:].reshape(B, S, MLSTM_HEADS, MLSTM_QK_DIM)
        mv = m_v.astype(f32).reshape(B, S, MLSTM_HEADS, MLSTM_V_DIM)
        i_pre = m_i.astype(f32) + b_if[l, :MLSTM_HEADS].astype(f32)
        f_pre = m_f.astype(f32) + b_if[l, MLSTM_HEADS:].astype(f32)
        o_ml = _head_norm(_mlstm(mq, mk, mv, i_pre, f_pre), mlstm_norm_g[l]) * jax.nn.sigmoid(m_o.astype(f32))
        y_ml = o_ml.astype(dt) @ w_proj_mlstm[l]

        gate_a = jax.nn.sigmoid(g_a + b_merge[l, :D_MODEL])
        gate_b = jax.nn.sigmoid(g_b + b_merge[l, D_MODEL:])
        mix = (gate_a * y_ret + gate_b * y_ml) @ w_out[l]
        x = _layer_norm(DEEPNORM_ALPHA * x + mix, ln1_g[l], ln1_b[l])

        gu = x @ w_gate_up[l]
        hidden = jax.nn.silu(gu[..., :D_FF]) * gu[..., D_FF:]
        x = _layer_norm(DEEPNORM_ALPHA * x + hidden @ w_down[l], ln2_g[l], ln2_b[l])
    return x
```

```python
import contextlib
import math
import numpy as np
import concourse.bass as bass
import concourse.mybir as mybir
from concourse.bass_utils import run_bass_kernel_spmd

F32 = mybir.dt.float32
BF16 = mybir.dt.bfloat16
AF = mybir.ActivationFunctionType
ALU = mybir.AluOpType
AX = mybir.AxisListType

D = 1024
NCORES = 8
NSEG = 4
DEPTH = 4
HEADS = 4
DK = 128
DV = 256
L = 128
DFF = 2816
D_IN = 8200
EPS = 1e-5
ALPHA = (2 * DEPTH) ** 0.25
TT = 512
NS = TT // L
NST = 2060
NCONST = 800
NLP = 64
NSLOT = 5

O_RQ, O_RK, O_RV, O_RG, O_MQK, O_MV, O_MO, O_MI, O_GA, O_GB = 0, 512, 1024, 2048, 3072, 4096, 5120, 6144, 6152, 7176


class Sched:
    LIM = 16000

    def __init__(self, nc, es):
        self.nc = nc
        self.es = es
        self.eng = {"pe": nc.tensor, "act": nc.scalar, "dve": nc.vector, "pool": nc.gpsimd, "sp": nc.sync}
        self.cnt = {e: 0 for e in self.eng}
        self.sems = {e: [] for e in self.eng}
        self.seen = {e: {} for e in self.eng}
        self.dsem = {}
        self.dcnt = {}
        self.last_w = {}
        self.last_r = {}
        self.overlaps = {}
        self.nwaits = 0
        self.nops = 0

    def add_overlap(self, a_keys, b_keys):
        for a in a_keys:
            self.overlaps.setdefault(a, set()).update(b_keys)
        for b in b_keys:
            self.overlaps.setdefault(b, set()).update(a_keys)

    def _sem_for(self, e, c):
        idx = (c - 1) // self.LIM
        while len(self.sems[e]) <= idx:
            self.sems[e].append(self.es.enter_context(self.nc.semaphore(f"s_{e}_{len(self.sems[e])}")))
        return self.sems[e][idx], (c - 1) % self.LIM + 1

    def _deps(self, reads, writes):
        deps = {}

        def need(ev):
            if ev is not None:
                src, c = ev
                if deps.get(src, 0) < c:
                    deps[src] = c

        for k in reads:
            need(self.last_w.get(k))
        for k in writes:
            ks = [k] + list(self.overlaps.get(k, ()))
            for kk in ks:
                need(self.last_w.get(kk))
                for src, c in self.last_r.get(kk, {}).items():
                    need((src, c))
        return deps

    def _emit_waits(self, e, deps):
        eng = self.eng[e]
        for src, c in deps.items():
            if src == e and e == "pe":
                continue
            if self.seen[e].get(src, 0) >= c:
                continue
            if src in self.eng:
                sem, val = self._sem_for(src, c)
            else:
                sem, val = self.dsem[src], c
            eng.wait_ge(sem, val)
            self.nwaits += 1
            self.seen[e][src] = c

    def op(self, e, fn, reads=(), writes=()):
        deps = self._deps(reads, writes)
        self._emit_waits(e, deps)
        ins = fn(self.eng[e])
        self.cnt[e] += 1
        c = self.cnt[e]
        sem, _ = self._sem_for(e, c)
        ins.then_inc(sem, 1)
        self.nops += 1
        for k in reads:
            self.last_r.setdefault(k, {})[e] = c
        for k in writes:
            self.last_w[k] = (e, c)
            self.last_r[k] = {}
        return ins

    def dma(self, q, semname, out, in_, reads=(), writes=()):
        if semname not in self.dsem:
            self.dsem[semname] = self.es.enter_context(self.nc.semaphore("d_" + semname))
            self.dcnt[semname] = 0
        deps = self._deps(reads, writes)
        self._emit_waits(q, deps)
        ins = self.eng[q].dma_start(out=out, in_=in_)
        self.dcnt[semname] += 16
        c = self.dcnt[semname]
        ins.then_inc(self.dsem[semname], 16)
        self.nops += 1
        for k in reads:
            self.last_r.setdefault(k, {})[semname] = c
        for k in writes:
            self.last_w[k] = (semname, c)
            self.last_r[k] = {}
        return ins

    def final_wait(self, e, keys):
        deps = {}
        for k in keys:
            ev = self.last_w.get(k)
            if ev is not None and deps.get(ev[0], 0) < ev[1]:
                deps[ev[0]] = ev[1]
            for src, c in self.last_r.get(k, {}).items():
                if deps.get(src, 0) < c:
                    deps[src] = c
        self._emit_waits(e, deps)


def build_program(T, mode):
    assert T % TT == 0
    NT = T // TT
    full = mode == "B"
    nc = bass.Bass("TRN2", target_bir_lowering=False)
    dram = lambda n, s, k: nc.dram_tensor(n, s, F32, kind=k).ap()
    x_d = dram("x", [T, D], "ExternalInput")
    xh_d = dram("xh", [3, D], "ExternalInput")
    rope_d = dram("rope", [T, 1024], "ExternalInput")
    const_d = dram("consts", [128, NCONST], "ExternalInput")
    lp_d = dram("lp", [128, NLP], "ExternalInput")
    rows_d = dram("rows", [6, D], "ExternalInput")
    w_in_d = dram("w_in", [D, D_IN], "ExternalInput")
    if full:
        allst_d = dram("allst", [NCORES, 128, NST], "ExternalInput")
        w_pr_d = dram("w_pr", [D, D], "ExternalInput")
        w_pm_d = dram("w_pm", [D, D], "ExternalInput")
        w_out_d = dram("w_out", [D, D], "ExternalInput")
        w_gu_d = dram("w_gu", [D, 2 * DFF], "ExternalInput")
        w_dn_d = dram("w_dn", [DFF, D], "ExternalInput")
        y_d = dram("y", [T, D], "ExternalOutput")
    else:
        st_d = dram("st", [128, NST], "ExternalOutput")

    es = contextlib.ExitStack()
    with es:
        S = Sched(nc, es)
        sb = lambda n, s, d: es.enter_context(nc.sbuf_tensor("sb_" + n, s, d))
        consts = sb("consts", [128, NCONST], F32)
        identf = consts[:, 0:128]
        maskT4 = consts[:, 128:640]
        maskT = consts[:, 128:256]
        onesf = consts[:, 640:768]
        cd = consts[:, 768:772]
        gseg = consts[:, 772:776]
        onehot = consts[:, 776:784]
        epsc = consts[:, 784:785]
        lnkc = consts[:, 785:786]
        lp = sb("lp", [128, NLP], F32)
        bif = lp[:, 0:8]
        convw = lp[:, 8:40]
        convb = lp[:, 40:48]
        bmerge = lp[:, 48:64]
        identb = sb("identb", [128, 128], BF16)
        onesb = sb("onesb", [128, 128], BF16)
        grow = sb("grow", [128, 2, D], F32)
        lnrow = sb("lnrow", [128, 2, D], F32)
        rope = sb("rope", [128, 1024], F32)
        xtok = sb("xtok", [128, NS, D], F32)
        xT = sb("xT", [128, 8, TT], BF16)
        oTr = sb("oTr", [128, 8, TT], BF16)
        oTm = sb("oTm", [128, 8, TT], BF16)
        wring = sb("wring", [128, NSLOT, 8, 512], BF16)
        arena = sb("arena", [128, 24576], BF16)
        pre = sb("pre", [128, 2, 516], F32)
        convt = sb("convt", [128, 2, 512], F32)
        halo = sb("halo", [128, 8, 4], F32)
        xh = sb("xh", [3, D], F32)
        xhT = sb("xhT", [128, 8, 4], BF16)
        rtmp = sb("rtmp", [128, 4, 256], F32)
        gtmp = sb("gtmp", [128, 2, 512], F32)
        qkTr = sb("qkTr", [128, 2, 1024], BF16)
        PT = sb("PT", [128, 2, 512], BF16)
        ktm = sb("ktm", [128, 2, 512], BF16)
        o_n = sb("o_n", [128, D], F32)
        o_bf = sb("o_bf", [128, 2, D], BF16)
        Rh = sb("Rh", [128, HEADS, DV], F32)
        Rb = sb("Rb", [128, HEADS, DV], BF16)
        Cs = sb("Cs", [128, HEADS, DV], F32)
        Cb = sb("Cb", [128, HEADS, DV], BF16)
        sm = sb("sm", [128, 256], F32)
        smb = sb("smb", [128, 16], BF16)
        iftok = sb("iftok", [128, NS, 8], F32)
        bnst = sb("bnst", [128, 8, 6], F32)
        bnmv = sb("bnmv", [128, 4, 2], F32)
        ps = es.enter_context(nc.psum_tensor("ps", [128, 4096], F32))
        bank = lambda b: ps[:, b * 512:(b + 1) * 512]
        ps4b = bank(4).bitcast(BF16)

        av = lambda off, n: arena[:, off:off + n]
        qtok = av(0, 2048).rearrange("p (s c) -> p s c", s=NS)
        ktok = av(2048, 2048).rearrange("p (s c) -> p s c", s=NS)
        vtok = av(4096, 4096).rearrange("p (s c) -> p s c", s=NS)
        gsil = av(8192, 4096).rearrange("p (s c) -> p s c", s=NS)
        qkT = av(12288, 4096).rearrange("p (c t) -> p c t", c=8)
        mvtok = av(16384, 4096).rearrange("p (s c) -> p s c", s=NS)
        gsig = av(20480, 4096).rearrange("p (s c) -> p s c", s=NS)
        mixT = av(0, 4096).rearrange("p (c t) -> p c t", c=8)
        x1 = av(4096, 8192).bitcast(F32).rearrange("p (s c) -> p s c", s=NS)
        hT = av(12288, 11264).rearrange("p (c t) -> p c t", c=22)
        stbuf = av(0, 2 * 2 * NST).bitcast(F32).rearrange("p (b c) -> p b c", b=2)
        accs = av(8448, 2 * 2064).bitcast(F32)

        sk = lambda name: [f"{name}{i}" for i in range(NS)]
        S.add_overlap(["mixT"], sk("qtok") + sk("ktok"))
        S.add_overlap(sk("x1_"), sk("vtok") + sk("gsil"))
        S.add_overlap([f"hT{i}" for i in range(22)], [f"qkT{i}" for i in range(8)] + sk("mvtok") + sk("gsig"))
        S.add_overlap(["stbuf0", "stbuf1", "accs"], sk("qtok") + sk("ktok") + sk("vtok") + sk("gsil"))

        S.dma("sp", "c0", consts[:], const_d[:, :], writes=["consts"])
        S.dma("sp", "c1", lp[:], lp_d[:, :], writes=["lp"])
        S.dma("sp", "c2", grow[:, 0, :], rows_d[0:1, :].partition_broadcast(128), writes=["grow"])
        S.dma("sp", "c2", grow[:, 1, :], rows_d[1:2, :].partition_broadcast(128), writes=["grow"])
        S.dma("sp", "c3", xh[:], xh_d[:, :], writes=["xh"])
        S.op("dve", lambda e: e.tensor_copy(identb[:], identf), ["consts"], ["identb"])
        S.op("dve", lambda e: e.tensor_copy(onesb[:], onesf), ["consts"], ["onesb"])

        wstate = {"n": 0}

        def wload(src2d, r0, nk, c0, ncols):
            slot = wstate["n"] % NSLOT
            wstate["n"] += 1
            src = src2d[r0:r0 + nk * 128, c0:c0 + ncols].rearrange("(k p) c -> p k c", p=128)
            S.dma("pool", f"w{slot}", wring[:, slot, 0:nk, 0:ncols], src, writes=[f"w{slot}"])
            return slot

        C_LNF, C_A, C_G, C_AMG, C_KW, C_CC, C_M, C_EFL, C_NEGF, C_N, C_DEN, C_ADEN, C_R, C_R2 = 0, 4, 8, 12, 16, 20, 24, 28, 32, 36, 40, 44, 48, 52
        C_VH, C_SQ, C_RS, C_SC, C_NB, C_GM, C_D4, C_TMP, C_MG, C_AA, C_CA, C_T2, C_MN = 56, 60, 64, 68, 72, 76, 80, 84, 88, 92, 96, 100, 104
        smc = lambda c, n=4: sm[:, c:c + n]
        P_NEGB, P_NEGBE, P_AT, P_GB, P_DEN, P_UN, P_IF, P_HALO = 0, 4, 8, 136, 140, 144, 148, 180
        p3 = bank(3)

        big = {"n": 0}

        def bigbank():
            b = big["n"] % 3
            big["n"] += 1
            return b

        S.op("dve", lambda e: e.memset(Rh[:], 0.0), [], ["Rh"])
        S.op("dve", lambda e: e.memset(Cs[:], 0.0), [], ["Cs"])
        S.op("dve", lambda e: e.memset(sm[:], 0.0), [], ["sm_n", "sm_m", "sm_negf"])
        S.op("dve", lambda e: e.memset(halo[:], 0.0), [], ["halo"])
        S.op("dve", lambda e: e.memset(pre[:], 0.0), [], ["pre0", "pre1"])

        if full:
            accR = accs[:, 0:1024].rearrange("p (h e) -> p h e", h=HEADS)
            accC = accs[:, 1024:2048].rearrange("p (h e) -> p h e", h=HEADS)
            accn = accs[:, 2048:2052]
            accm = accs[:, 2052:2056]
            for bb in range(2):
                S.op("dve", lambda e: e.memset(accs[:], 0.0), [], ["accs"])
                for j in range(NSEG - 1):
                    r = bb * NSEG + j
                    sbi = (bb * (NSEG - 1) + j) % 2
                    stb = stbuf[:, sbi, :]
                    S.dma("sp", f"st{sbi}", stb, allst_d[r, :, :], writes=[f"stbuf{sbi}"])
                    rk = [f"stbuf{sbi}", "accs", "consts"]
                    Rl = stb[:, 0:1024].rearrange("p (h e) -> p h e", h=HEADS)
                    Cl = stb[:, 1024:2048].rearrange("p (h e) -> p h e", h=HEADS)
                    nl, ml, nfl = stb[:, 2048:2052], stb[:, 2052:2056], stb[:, 2056:2060]
                    for h in range(HEADS):
                        S.op("dve", lambda e, h=h: e.scalar_tensor_tensor(accR[:, h, :], accR[:, h, :], gseg[:, h:h + 1], Rl[:, h, :], op0=ALU.mult, op1=ALU.add), rk, ["accs"])
                    S.op("dve", lambda e: e.tensor_tensor(smc(C_T2), accm, nfl, op=ALU.subtract), rk, ["sm_t2"])
                    S.op("dve", lambda e: e.tensor_tensor(smc(C_MN), smc(C_T2), ml, op=ALU.max), rk + ["sm_t2"], ["sm_mn"])
                    S.op("dve", lambda e: e.tensor_tensor(smc(C_AA), smc(C_T2), smc(C_MN), op=ALU.subtract), ["sm_t2", "sm_mn"], ["sm_aa"])
                    S.op("dve", lambda e: e.tensor_tensor(smc(C_CA), ml, smc(C_MN), op=ALU.subtract), rk + ["sm_mn"], ["sm_ca"])
                    S.op("act", lambda e: e.activation(out=smc(C_AA), in_=smc(C_AA), func=AF.Exp), ["sm_aa"], ["sm_aa"])
                    S.op("act", lambda e: e.activation(out=smc(C_CA), in_=smc(C_CA), func=AF.Exp), ["sm_ca"], ["sm_ca"])
                    for h in range(HEADS):
                        S.op("dve", lambda e, h=h: e.tensor_scalar(Cl[:, h, :], Cl[:, h, :], smc(C_CA)[:, h:h + 1], None, op0=ALU.mult), rk + ["sm_ca"], [f"stbuf{sbi}"])
                        S.op("dve", lambda e, h=h: e.scalar_tensor_tensor(accC[:, h, :], accC[:, h, :], smc(C_AA)[:, h:h + 1], Cl[:, h, :], op0=ALU.mult, op1=ALU.add), rk + ["sm_aa"], ["accs"])
                    S.op("dve", lambda e: e.tensor_tensor(nl, nl, smc(C_CA), op=ALU.mult), rk + ["sm_ca"], [f"stbuf{sbi}"])
                    S.op("dve", lambda e: e.tensor_tensor(accn, accn, smc(C_AA), op=ALU.mult), rk + ["sm_aa"], ["accs"])
                    S.op("dve", lambda e: e.tensor_tensor(accn, accn, nl, op=ALU.add), rk, ["accs"])
                    S.op("dve", lambda e: e.tensor_copy(accm, smc(C_MN)), ["sm_mn"], ["accs"])
                    oh = onehot[:, r + 1:r + 2]
                    RhF = Rh[:].rearrange("p h e -> p (h e)")
                    CsF = Cs[:].rearrange("p h e -> p (h e)")
                    S.op("dve", lambda e: e.scalar_tensor_tensor(RhF, accs[:, 0:1024], oh, RhF, op0=ALU.mult, op1=ALU.add), rk + ["Rh"], ["Rh"])
                    S.op("dve", lambda e: e.scalar_tensor_tensor(CsF, accs[:, 1024:2048], oh, CsF, op0=ALU.mult, op1=ALU.add), rk + ["Cs"], ["Cs"])
                    S.op("dve", lambda e: e.scalar_tensor_tensor(smc(C_N), accn, oh, smc(C_N), op0=ALU.mult, op1=ALU.add), rk + ["sm_n"], ["sm_n"])
                    S.op("dve", lambda e: e.scalar_tensor_tensor(smc(C_M), accm, oh, smc(C_M), op0=ALU.mult, op1=ALU.add), rk + ["sm_m"], ["sm_m"])
            for h in range(HEADS):
                S.op("act", lambda e, h=h: e.activation(out=Rb[:, h, :], in_=Rh[:, h, :], func=AF.Copy, scale=cd[:, h:h + 1]), ["Rh", "consts"], ["Rb"])

        for kc in range(8):
            S.op("pe", lambda e, kc=kc: e.transpose(p3[:, P_HALO + kc * 4:P_HALO + kc * 4 + 3], xh[0:3, kc * 128:(kc + 1) * 128], identf[0:3, 0:3]), ["xh", "consts"], ["p3_halo"])
        S.op("dve", lambda e: e.memset(xhT[:], 0.0), [], ["xhT"])
        S.op("dve", lambda e: e.tensor_copy(xhT[:, :, 0:3], p3[:, P_HALO:P_HALO + 32].rearrange("p (k c) -> p k c", k=8)[:, :, 0:3]), ["p3_halo"], ["xhT"])

        def make_xT(src_tok, src_keys):
            for kc in range(8):
                b = bigbank()
                for s in range(NS):
                    S.op("pe", lambda e, s=s, kc=kc, b=b: e.transpose(bank(b)[:, s * 128:(s + 1) * 128], src_tok[:, s, kc * 128:(kc + 1) * 128], identf), [src_keys[s], "consts"], [f"ps{b}"])
                S.op("act", lambda e, kc=kc, b=b: e.copy(xT[:, kc, :], bank(b)), [f"ps{b}"], [f"xT{kc}"])

        def tokmajor_piece(slot, ncols, evac):
            for s in range(NS):
                b = bigbank()
                for kc in range(8):
                    S.op("pe", lambda e, s=s, kc=kc, b=b: e.matmul(bank(b)[:, 0:ncols], lhsT=xT[:, kc, s * 128:(s + 1) * 128], rhs=wring[:, slot, kc, 0:ncols], start=(kc == 0), stop=(kc == 7)),
                         [f"xT{kc}", f"w{slot}"], [f"ps{b}"])
                evac(s, b)

        def rotary_evac(dst, dstname, coff):
            def f(s, b):
                src = bank(b).rearrange("p (h t f) -> p h t f", h=HEADS, t=2)
                x1v, x2v = src[:, :, 0, :], src[:, :, 1, :]
                cq = rope[:, coff:coff + 256].rearrange("p (h f) -> p h f", h=HEADS)
                sq = rope[:, coff + 256:coff + 512].rearrange("p (h f) -> p h f", h=HEADS)
                out = dst[:, s, :].rearrange("p (h t f) -> p h t f", h=HEADS, t=2)
                t = [rtmp[:, i, :].rearrange("p (h f) -> p h f", h=HEADS) for i in range(4)]
                rd = [f"ps{b}", "rope"]
                S.op("dve", lambda e: e.tensor_tensor(t[0], x1v, cq, op=ALU.mult), rd, ["rt0"])
                S.op("dve", lambda e: e.tensor_tensor(t[1], x2v, sq, op=ALU.mult), rd, ["rt1"])
                S.op("dve", lambda e: e.tensor_tensor(t[2], x1v, sq, op=ALU.mult), rd, ["rt2"])
                S.op("dve", lambda e: e.tensor_tensor(t[3], x2v, cq, op=ALU.mult), rd, ["rt3"])
                S.op("dve", lambda e: e.tensor_tensor(out[:, :, 0, :], t[0], t[1], op=ALU.subtract), ["rt0", "rt1"], [f"{dstname}{s}"])
                S.op("dve", lambda e: e.tensor_tensor(out[:, :, 1, :], t[2], t[3], op=ALU.add), ["rt2", "rt3"], [f"{dstname}{s}"])
            return f

        def head_norm_tail(s, sc_ap, nb_ap, gate, gatename, oT, oTname, par):
            for h in range(HEADS):
                S.op("act", lambda e, h=h: e.activation(out=o_n[:, h * DV:(h + 1) * DV], in_=ps[:, 3072 + h * DV:3072 + (h + 1) * DV], func=AF.Identity, bias=nb_ap[:, h:h + 1], scale=sc_ap[:, h:h + 1]),
                     ["ps6", "sm_sc", "sm_nb"], ["o_n"])
            S.op("dve", lambda e: e.tensor_tensor(o_bf[:, par, :], o_n[:], gate[:, s, :], op=ALU.mult), ["o_n", f"{gatename}{s}"], [f"o_bf{par}"])
            for kc in range(8):
                S.op("pe", lambda e, kc=kc: e.transpose(ps4b[:, kc * 128:(kc + 1) * 128], o_bf[:, par, kc * 128:(kc + 1) * 128], identb[:]), [f"o_bf{par}", "identb"], ["ps4"])
            S.op("act", lambda e: e.copy(oT[:, :, s * 128:(s + 1) * 128], ps4b.rearrange("p (k t) -> p k t", k=8)), ["ps4"], [oTname])

        def stats4(tag):
            for h in range(HEADS):
                S.op("dve", lambda e, h=h: e.bn_stats(bnst[:, h, :], ps[:, 3072 + h * DV:3072 + (h + 1) * DV]), ["ps6"], [f"bnst{h}"])
                S.op("dve", lambda e, h=h: e.bn_aggr(bnmv[:, h, :], bnst[:, h, :]), [f"bnst{h}"], ["bnmv"])

        for ti in range(NT):
            t0 = ti * TT
            S.dma("sp", "xld", xtok[:], x_d[t0:t0 + TT, :].rearrange("(s p) d -> p s d", p=128), writes=sk("xtok"))
            make_xT(xtok, sk("xtok"))

            def evac_copy(dst, dstname, c0):
                def f(s, b):
                    S.op("act", lambda e: e.copy(dst[:, s, c0:c0 + 512], bank(b)), [f"ps{b}"], [f"{dstname}{s}"])
                return f

            def evac_gate(dst, dstname, c0, func, gi):
                def f(s, b):
                    gt = gtmp[:, s % 2, :]
                    S.op("act", lambda e: e.activation(out=gt, in_=bank(b), func=func), [f"ps{b}"], [f"gtmp{s % 2}"])
                    S.op("dve", lambda e: e.tensor_tensor(dst[:, s, c0:c0 + 512], gt, grow[:, gi, c0:c0 + 512], op=ALU.mult), [f"gtmp{s % 2}", "grow"], [f"{dstname}{s}"])
                return f

            slot_q = wload(w_in_d, 0, 8, O_RQ, 512) if full else None
            slot_k = wload(w_in_d, 0, 8, O_RK, 512)
            for s in range(NS):
                S.dma("sp", "rope", rope[:], rope_d[t0 + s * 128:t0 + (s + 1) * 128, :], writes=["rope"])
                for (slot, dst, dstname, coff) in ([(slot_q, qtok, "qtok", 0)] if full else []) + [(slot_k, ktok, "ktok", 512)]:
                    b = bigbank()
                    for kc in range(8):
                        S.op("pe", lambda e, s=s, kc=kc, b=b, slot=slot: e.matmul(bank(b), lhsT=xT[:, kc, s * 128:(s + 1) * 128], rhs=wring[:, slot, kc, :], start=(kc == 0), stop=(kc == 7)),
                             [f"xT{kc}", f"w{slot}"], [f"ps{b}"])
                    rotary_evac(dst, dstname, coff)(s, b)
            for half in range(2):
                slot = wload(w_in_d, 0, 8, O_RV + half * 512, 512)
                tokmajor_piece(slot, 512, evac_copy(vtok, "vtok", half * 512))
            if full:
                for half in range(2):
                    slot = wload(w_in_d, 0, 8, O_RG + half * 512, 512)
                    tokmajor_piece(slot, 512, evac_gate(gsil, "gsil", half * 512, AF.Silu, 0))
            for half in ([0, 1] if full else [1]):
                slot = wload(w_in_d, 0, 8, O_MQK + half * 512, 512)
                for j in range(4):
                    cc = half * 4 + j
                    b = bigbank()
                    for kc in range(8):
                        S.op("pe", lambda e, kc=kc, b=b, j=j, slot=slot: e.matmul(bank(b), lhsT=wring[:, slot, kc, j * 128:(j + 1) * 128], rhs=xT[:, kc, :], start=(kc == 0), stop=(kc == 7)),
                             [f"xT{kc}", f"w{slot}"], [f"ps{b}"])
                    if ti == 0:
                        for kc in range(8):
                            S.op("pe", lambda e, kc=kc, j=j, cc=cc, slot=slot: e.matmul(p3[:, P_HALO + cc * 4:P_HALO + cc * 4 + 3], lhsT=wring[:, slot, kc, j * 128:(j + 1) * 128], rhs=xhT[:, kc, 0:3], start=(kc == 0), stop=(kc == 7)),
                                 ["xhT", f"w{slot}"], ["p3_halo"])
                        S.op("act", lambda e, cc=cc: e.copy(halo[:, cc, 0:3], p3[:, P_HALO + cc * 4:P_HALO + cc * 4 + 3]), ["p3_halo"], ["halo"])
                    pb = cc % 2
                    S.op("act", lambda e, b=b, pb=pb: e.copy(pre[:, pb, 3:515], bank(b)), [f"ps{b}"], [f"pre{pb}"])
                    S.op("act", lambda e, cc=cc, pb=pb: e.copy(pre[:, pb, 0:3], halo[:, cc, 0:3]), ["halo"], [f"pre{pb}"])
                    S.op("act", lambda e, cc=cc, pb=pb: e.copy(halo[:, cc, 0:3], pre[:, pb, 512:515]), [f"pre{pb}"], ["halo"])
                    cw = lambda k, cc=cc: convw[:, cc * 4 + k:cc * 4 + k + 1]
                    S.op("dve", lambda e, cc=cc, pb=pb: e.tensor_scalar(convt[:, 0, :], pre[:, pb, 3:515], cw(3), convb[:, cc:cc + 1], op0=ALU.mult, op1=ALU.add), [f"pre{pb}", "lp"], ["convt0"])
                    S.op("dve", lambda e, pb=pb: e.scalar_tensor_tensor(convt[:, 1, :], pre[:, pb, 2:514], cw(2), convt[:, 0, :], op0=ALU.mult, op1=ALU.add), [f"pre{pb}", "lp", "convt0"], ["convt1"])
                    S.op("dve", lambda e, pb=pb: e.scalar_tensor_tensor(convt[:, 0, :], pre[:, pb, 1:513], cw(1), convt[:, 1, :], op0=ALU.mult, op1=ALU.add), [f"pre{pb}", "lp", "convt1"], ["convt0"])
                    S.op("dve", lambda e, pb=pb: e.scalar_tensor_tensor(convt[:, 1, :], pre[:, pb, 0:512], cw(0), convt[:, 0, :], op0=ALU.mult, op1=ALU.add), [f"pre{pb}", "lp", "convt0"], ["convt1"])
                    S.op("act", lambda e, cc=cc: e.activation(out=qkT[:, cc, :], in_=convt[:, 1, :], func=AF.Silu), ["convt1"], [f"qkT{cc}"])
            slot = wload(w_in_d, 0, 8, O_MI, 8)
            for s in range(NS):
                for kc in range(8):
                    S.op("pe", lambda e, s=s, kc=kc, slot=slot: e.matmul(p3[:, P_IF + s * 8:P_IF + s * 8 + 8], lhsT=xT[:, kc, s * 128:(s + 1) * 128], rhs=wring[:, slot, kc, 0:8], start=(kc == 0), stop=(kc == 7)),
                         [f"xT{kc}", f"w{slot}"], ["p3_if"])
                S.op("dve", lambda e, s=s: e.tensor_tensor(iftok[:, s, :], p3[:, P_IF + s * 8:P_IF + s * 8 + 8], bif, op=ALU.add), ["p3_if", "lp"], [f"iftok{s}"])
            for half in range(2):
                slot = wload(w_in_d, 0, 8, O_MV + half * 512, 512)
                tokmajor_piece(slot, 512, evac_copy(mvtok, "mvtok", half * 512))
            if full:
                for half in range(2):
                    slot = wload(w_in_d, 0, 8, O_MO + half * 512, 512)
                    tokmajor_piece(slot, 512, evac_gate(gsig, "gsig", half * 512, AF.Sigmoid, 1))

            for s in range(NS):
                par = s % 2
                if full:
                    for h in range(HEADS):
                        S.op("pe", lambda e, h=h: e.transpose(ps4b[:, h * 128:(h + 1) * 128], qtok[:, s, h * 128:(h + 1) * 128], identb[:]), [f"qtok{s}", "identb"], ["ps4"])
                        S.op("pe", lambda e, h=h: e.transpose(ps4b[:, (4 + h) * 128:(5 + h) * 128], ktok[:, s, h * 128:(h + 1) * 128], identb[:]), [f"ktok{s}", "identb"], ["ps4"])
                    S.op("act", lambda e: e.copy(qkTr[:, par, :], ps4b), ["ps4"], [f"qkTr{par}"])
                    qTr = lambda h: qkTr[:, par, h * 128:(h + 1) * 128]
                    kTr = lambda h: qkTr[:, par, (4 + h) * 128:(5 + h) * 128]
                    for h in range(HEADS):
                        S.op("pe", lambda e, h=h: e.matmul(bank(5)[:, h * 128:(h + 1) * 128], lhsT=kTr(h), rhs=qTr(h), start=True, stop=True), [f"qkTr{par}"], ["ps5"])
                    S.op("dve", lambda e: e.tensor_tensor(PT[:, par, :], bank(5), maskT4, op=ALU.mult), ["ps5", "consts"], [f"PT{par}"])
                    for h in range(HEADS):
                        S.op("pe", lambda e, h=h: e.matmul(ps[:, 3072 + h * DV:3072 + (h + 1) * DV], lhsT=PT[:, par, h * 128:(h + 1) * 128], rhs=vtok[:, s, h * DV:(h + 1) * DV], start=True, stop=False), [f"PT{par}", f"vtok{s}"], ["ps6"])
                        S.op("pe", lambda e, h=h: e.matmul(ps[:, 3072 + h * DV:3072 + (h + 1) * DV], lhsT=qTr(h), rhs=Rb[:, h, :], start=False, stop=True), [f"qkTr{par}", "Rb"], ["ps6"])
                for hp in range(2):
                    b = bigbank()
                    for hh in range(2):
                        h = hp * 2 + hh
                        S.op("pe", lambda e, h=h, hh=hh, b=b: e.matmul(bank(b)[:, hh * DV:(hh + 1) * DV], lhsT=ktok[:, s, h * 128:(h + 1) * 128], rhs=vtok[:, s, h * DV:(h + 1) * DV], start=True, stop=True), [f"ktok{s}", f"vtok{s}"], [f"ps{b}"])
                    for hh in range(2):
                        h = hp * 2 + hh
                        S.op("dve", lambda e, h=h, hh=hh, b=b: e.scalar_tensor_tensor(Rh[:, h, :], Rh[:, h, :], cd[:, h:h + 1], bank(b)[:, hh * DV:(hh + 1) * DV], op0=ALU.mult, op1=ALU.add), ["Rh", "consts", f"ps{b}"], ["Rh"])
                if full:
                    for h in range(HEADS):
                        S.op("act", lambda e, h=h: e.activation(out=Rb[:, h, :], in_=Rh[:, h, :], func=AF.Copy, scale=cd[:, h:h + 1]), ["Rh", "consts"], ["Rb"])
                    stats4("r")
                    S.op("act", lambda e: e.activation(out=smc(C_SQ), in_=bnmv[:, :, 1], func=AF.Sqrt, bias=epsc, scale=1.0), ["bnmv", "consts"], ["sm_sq"])
                    S.op("dve", lambda e: e.reciprocal(smc(C_SC), smc(C_SQ)), ["sm_sq"], ["sm_sc"])
                    S.op("dve", lambda e: e.scalar_tensor_tensor(smc(C_NB), bnmv[:, :, 0], -1.0, smc(C_SC), op0=ALU.mult, op1=ALU.mult), ["bnmv", "sm_sc"], ["sm_nb"])
                    head_norm_tail(s, smc(C_SC), smc(C_NB), gsil, "gsil", oTr, "oTr", par)

                ip = iftok[:, s, 0:4]
                fp = iftok[:, s, 4:8]
                S.op("act", lambda e: e.activation(out=smc(C_LNF), in_=fp, func=AF.Exp, scale=-1.0), [f"iftok{s}"], ["sm_lnf"])
                S.op("act", lambda e: e.activation(out=smc(C_LNF), in_=smc(C_LNF), func=AF.Ln, bias=consts[:, 786:787], scale=1.0), ["sm_lnf", "consts"], ["sm_lnf"])
                S.op("pe", lambda e: e.matmul(p3[:, P_NEGB:P_NEGB + 4], lhsT=maskT, rhs=smc(C_LNF), start=True, stop=True), ["consts", "sm_lnf"], ["p3_negb"])
                S.op("pe", lambda e: e.matmul(p3[:, P_NEGBE:P_NEGBE + 4], lhsT=onesf, rhs=smc(C_LNF), start=True, stop=True), ["consts", "sm_lnf"], ["p3_negbe"])
                S.op("dve", lambda e: e.tensor_tensor(smc(C_A), ip, p3[:, P_NEGB:P_NEGB + 4], op=ALU.add), [f"iftok{s}", "p3_negb"], ["sm_a"])
                S.op("pe", lambda e: e.transpose(p3[0:4, P_AT:P_AT + 128], smc(C_A), identf), ["sm_a", "consts"], ["p3_at"])
                S.op("dve", lambda e: e.tensor_reduce(sm[0:4, C_GM:C_GM + 1], p3[0:4, P_AT:P_AT + 128], axis=AX.X, op=ALU.max), ["p3_at"], ["sm_gm"])
                S.op("dve", lambda e: e.tensor_scalar(sm[0:4, C_D4:C_D4 + 4], identf[0:4, 0:4], sm[0:4, C_GM:C_GM + 1], None, op0=ALU.mult), ["sm_gm", "consts"], ["sm_d4"])
                S.op("pe", lambda e: e.matmul(p3[:, P_GB:P_GB + 4], lhsT=onesf[0:4, :], rhs=sm[0:4, C_D4:C_D4 + 4], start=True, stop=True), ["consts", "sm_d4"], ["p3_gb"])
                S.op("dve", lambda e: e.tensor_tensor(smc(C_G), p3[:, P_GB:P_GB + 4], smc(C_M), op=ALU.max), ["p3_gb", "sm_m"], ["sm_g"])
                S.op("dve", lambda e: e.tensor_tensor(smc(C_AMG), smc(C_A), smc(C_G), op=ALU.subtract), ["sm_a", "sm_g"], ["sm_amg"])
                S.op("act", lambda e: e.activation(out=smc(C_KW), in_=smc(C_AMG), func=AF.Exp, bias=lnkc, scale=1.0), ["sm_amg", "consts"], ["sm_kw"])
                S.op("dve", lambda e: e.tensor_tensor(smc(C_CC), smc(C_M), smc(C_G), op=ALU.subtract), ["sm_m", "sm_g"], ["sm_cc"])
                S.op("act", lambda e: e.activation(out=smc(C_CC), in_=smc(C_CC), func=AF.Exp), ["sm_cc"], ["sm_cc"])
                S.op("dve", lambda e: e.tensor_tensor(smc(C_M), smc(C_G), p3[:, P_NEGBE:P_NEGBE + 4], op=ALU.subtract), ["sm_g", "p3_negbe"], ["sm_m"])
                S.op("dve", lambda e: e.tensor_tensor(smc(C_NEGF), smc(C_NEGF), p3[:, P_NEGBE:P_NEGBE + 4], op=ALU.add), ["sm_negf", "p3_negbe"], ["sm_negf"])
                if full:
                    S.op("dve", lambda e: e.tensor_tensor(smc(C_EFL), p3[:, P_NEGB:P_NEGB + 4], smc(C_G), op=ALU.subtract), ["p3_negb", "sm_g"], ["sm_efl"])
                    S.op("act", lambda e: e.activation(out=smc(C_EFL), in_=smc(C_EFL), func=AF.Exp), ["sm_efl"], ["sm_efl"])
                for h in range(HEADS):
                    S.op("pe", lambda e, h=h: e.transpose(ps4b[:, h * 128:(h + 1) * 128], qkT[:, 4 + h, s * 128:(s + 1) * 128], identb[:]), [f"qkT{4 + h}", "identb"], ["ps4"])
                for h in range(HEADS):
                    S.op("act", lambda e, h=h: e.activation(out=ktm[:, par, h * 128:(h + 1) * 128], in_=ps4b[:, h * 128:(h + 1) * 128], func=AF.Copy, scale=smc(C_KW)[:, h:h + 1]), ["ps4", "sm_kw"], [f"ktm{par}"])
                for h in range(HEADS):
                    S.op("dve", lambda e, h=h: e.tensor_scalar(Cs[:, h, :], Cs[:, h, :], smc(C_CC)[:, h:h + 1], None, op0=ALU.mult), ["Cs", "sm_cc"], ["Cs"])
                S.op("dve", lambda e: e.tensor_tensor(smc(C_N), smc(C_N), smc(C_CC), op=ALU.mult), ["sm_n", "sm_cc"], ["sm_n"])
                if full:
                    S.op("act", lambda e: e.copy(Cb[:].rearrange("p h e -> p (h e)"), Cs[:].rearrange("p h e -> p (h e)")), ["Cs"], ["Cb"])
                    S.op("act", lambda e: e.copy(smb[:, 0:4], smc(C_N)), ["sm_n"], ["smb_n"])
                    mq = lambda h: qkT[:, h, s * 128:(s + 1) * 128]
                    mk = lambda h: qkT[:, 4 + h, s * 128:(s + 1) * 128]
                    for h in range(HEADS):
                        S.op("pe", lambda e, h=h: e.matmul(bank(5)[:, h * 128:(h + 1) * 128], lhsT=mk(h), rhs=mq(h), start=True, stop=True), [f"qkT{h}", f"qkT{4 + h}"], ["ps5"])
                    for h in range(HEADS):
                        S.op("dve", lambda e, h=h: e.scalar_tensor_tensor(PT[:, par, h * 128:(h + 1) * 128], bank(5)[:, h * 128:(h + 1) * 128], smc(C_KW)[:, h:h + 1], maskT, op0=ALU.mult, op1=ALU.mult), ["ps5", "sm_kw", "consts"], [f"PT{par}"])
                    for h in range(HEADS):
                        S.op("pe", lambda e, h=h: e.matmul(ps[:, 3072 + h * DV:3072 + (h + 1) * DV], lhsT=PT[:, par, h * 128:(h + 1) * 128], rhs=mvtok[:, s, h * DV:(h + 1) * DV], start=True, stop=False), [f"PT{par}", f"mvtok{s}"], ["ps6"])
                        S.op("pe", lambda e, h=h: e.matmul(ps[:, 3072 + h * DV:3072 + (h + 1) * DV], lhsT=mq(h), rhs=Cb[:, h, :], start=False, stop=True), [f"qkT{h}", "Cb"], ["ps6"])
                        S.op("pe", lambda e, h=h: e.matmul(p3[:, P_DEN + h:P_DEN + h + 1], lhsT=PT[:, par, h * 128:(h + 1) * 128], rhs=onesb[:, 0:1], start=True, stop=False), [f"PT{par}", "onesb"], ["p3_den"])
                        S.op("pe", lambda e, h=h: e.matmul(p3[:, P_DEN + h:P_DEN + h + 1], lhsT=mq(h), rhs=smb[:, h:h + 1], start=False, stop=True), [f"qkT{h}", "smb_n"], ["p3_den"])
                for hp in range(2):
                    b = bigbank()
                    for hh in range(2):
                        h = hp * 2 + hh
                        S.op("pe", lambda e, h=h, hh=hh, b=b: e.matmul(bank(b)[:, hh * DV:(hh + 1) * DV], lhsT=ktm[:, par, h * 128:(h + 1) * 128], rhs=mvtok[:, s, h * DV:(h + 1) * DV], start=True, stop=True), [f"ktm{par}", f"mvtok{s}"], [f"ps{b}"])
                    S.op("dve", lambda e, hp=hp, b=b: e.tensor_tensor(Cs[:, hp * 2:hp * 2 + 2, :].rearrange("p h e -> p (h e)"), Cs[:, hp * 2:hp * 2 + 2, :].rearrange("p h e -> p (h e)"), bank(b), op=ALU.add), ["Cs", f"ps{b}"], ["Cs"])
                for h in range(HEADS):
                    S.op("pe", lambda e, h=h: e.matmul(p3[:, P_UN + h:P_UN + h + 1], lhsT=ktm[:, par, h * 128:(h + 1) * 128], rhs=onesb[:, 0:1], start=True, stop=True), [f"ktm{par}", "onesb"], ["p3_un"])
                S.op("dve", lambda e: e.tensor_tensor(smc(C_N), smc(C_N), p3[:, P_UN:P_UN + 4], op=ALU.add), ["sm_n", "p3_un"], ["sm_n"])
                if full:
                    S.op("act", lambda e: e.copy(smc(C_DEN), p3[:, P_DEN:P_DEN + 4]), ["p3_den"], ["sm_den"])
                    S.op("dve", lambda e: e.scalar_tensor_tensor(smc(C_ADEN), smc(C_DEN), -1.0, smc(C_DEN), op0=ALU.mult, op1=ALU.max), ["sm_den"], ["sm_aden"])
                    S.op("dve", lambda e: e.tensor_tensor(smc(C_ADEN), smc(C_ADEN), smc(C_EFL), op=ALU.max), ["sm_aden", "sm_efl"], ["sm_aden"])
                    S.op("dve", lambda e: e.reciprocal(smc(C_R), smc(C_ADEN)), ["sm_aden"], ["sm_r"])
                    stats4("m")
                    S.op("dve", lambda e: e.tensor_tensor(smc(C_R2), smc(C_R), smc(C_R), op=ALU.mult), ["sm_r"], ["sm_r2"])
                    S.op("dve", lambda e: e.tensor_tensor(smc(C_VH), bnmv[:, :, 1], smc(C_R2), op=ALU.mult), ["bnmv", "sm_r2"], ["sm_vh"])
                    S.op("act", lambda e: e.activation(out=smc(C_SQ), in_=smc(C_VH), func=AF.Sqrt, bias=epsc, scale=1.0), ["sm_vh", "consts"], ["sm_sq"])
                    S.op("dve", lambda e: e.reciprocal(smc(C_RS), smc(C_SQ)), ["sm_sq"], ["sm_rs"])
                    S.op("dve", lambda e: e.tensor_tensor(smc(C_SC), smc(C_R), smc(C_RS), op=ALU.mult), ["sm_r", "sm_rs"], ["sm_sc"])
                    S.op("dve", lambda e: e.scalar_tensor_tensor(smc(C_NB), bnmv[:, :, 0], -1.0, smc(C_SC), op0=ALU.mult, op1=ALU.mult), ["bnmv", "sm_sc"], ["sm_nb"])
                    head_norm_tail(s, smc(C_SC), smc(C_NB), gsig, "gsig", oTm, "oTm", par)

            if not full:
                continue

            def fm_group(slot, j, rhsT, rkey):
                b = bigbank()
                for kc in range(8):
                    S.op("pe", lambda e, kc=kc, b=b: e.matmul(bank(b), lhsT=wring[:, slot, kc, j * 128:(j + 1) * 128], rhs=rhsT[:, kc, :], start=(kc == 0), stop=(kc == 7)),
                         [f"w{slot}", rkey(kc)], [f"ps{b}"])
                return b

            for half in range(2):
                s_ga = wload(w_in_d, 0, 8, O_GA + half * 512, 512)
                s_pr = wload(w_pr_d, 0, 8, half * 512, 512)
                s_gb = wload(w_in_d, 0, 8, O_GB + half * 512, 512)
                s_pm = wload(w_pm_d, 0, 8, half * 512, 512)
                for j in range(4):
                    cc = half * 4 + j
                    b = fm_group(s_ga, j, xT, lambda kc: f"xT{kc}")
                    S.op("act", lambda e, b=b, cc=cc: e.activation(out=gtmp[:, 0, :], in_=bank(b), func=AF.Sigmoid, bias=bmerge[:, cc:cc + 1], scale=1.0), [f"ps{b}", "lp"], ["gtmp0"])
                    b = fm_group(s_pr, j, oTr, lambda kc: "oTr")
                    S.op("dve", lambda e, b=b: e.tensor_tensor(convt[:, 0, :], gtmp[:, 0, :], bank(b), op=ALU.mult), ["gtmp0", f"ps{b}"], ["convt0"])
                    b = fm_group(s_gb, j, xT, lambda kc: f"xT{kc}")
                    S.op("act", lambda e, b=b, cc=cc: e.activation(out=gtmp[:, 1, :], in_=bank(b), func=AF.Sigmoid, bias=bmerge[:, 8 + cc:9 + cc], scale=1.0), [f"ps{b}", "lp"], ["gtmp1"])
                    b = fm_group(s_pm, j, oTm, lambda kc: "oTm")
                    S.op("dve", lambda e, b=b: e.tensor_tensor(convt[:, 1, :], gtmp[:, 1, :], bank(b), op=ALU.mult), ["gtmp1", f"ps{b}"], ["convt1"])
                    S.op("dve", lambda e, cc=cc: e.tensor_tensor(mixT[:, cc, :], convt[:, 0, :], convt[:, 1, :], op=ALU.add), ["convt0", "convt1"], ["mixT"])

            def layer_norm(src, dst, srckeys, dstkeys, s):
                for hf in range(2):
                    S.op("dve", lambda e, hf=hf: e.bn_stats(bnst[:, hf, :], src[:, s, hf * 512:(hf + 1) * 512]), [srckeys[s]], [f"bnst{hf}"])
                S.op("dve", lambda e: e.bn_aggr(bnmv[:, 0, :], bnst[:, 0:2, :].rearrange("p a b -> p (a b)")), ["bnst0", "bnst1"], ["bnmv"])
                S.op("act", lambda e: e.activation(out=sm[:, C_SQ:C_SQ + 1], in_=bnmv[:, 0, 1:2], func=AF.Sqrt, bias=epsc, scale=1.0), ["bnmv", "consts"], ["sm_sq"])
                S.op("dve", lambda e: e.reciprocal(sm[:, C_SC:C_SC + 1], sm[:, C_SQ:C_SQ + 1]), ["sm_sq"], ["sm_sc"])
                S.op("dve", lambda e: e.scalar_tensor_tensor(sm[:, C_NB:C_NB + 1], bnmv[:, 0, 0:1], -1.0, sm[:, C_SC:C_SC + 1], op0=ALU.mult, op1=ALU.mult), ["bnmv", "sm_sc"], ["sm_nb"])
                S.op("act", lambda e: e.activation(out=dst[:, s, :], in_=src[:, s, :], func=AF.Identity, bias=sm[:, C_NB:C_NB + 1], scale=sm[:, C_SC:C_SC + 1]), [srckeys[s], "sm_sc", "sm_nb"], [dstkeys[s]])
                S.op("dve", lambda e: e.tensor_tensor(dst[:, s, :], dst[:, s, :], lnrow[:, 0, :], op=ALU.mult), [dstkeys[s], "lnrow"], [dstkeys[s]])
                S.op("dve", lambda e: e.tensor_tensor(dst[:, s, :], dst[:, s, :], lnrow[:, 1, :], op=ALU.add), [dstkeys[s], "lnrow"], [dstkeys[s]])

            S.dma("sp", "lnrow", lnrow[:, 0, :], rows_d[2:3, :].partition_broadcast(128), writes=["lnrow"])
            S.dma("sp", "lnrow", lnrow[:, 1, :], rows_d[3:4, :].partition_broadcast(128), writes=["lnrow"])
            for half in range(2):
                slot = wload(w_out_d, 0, 8, half * 512, 512)
                for s in range(NS):
                    b = bigbank()
                    for kc in range(8):
                        S.op("pe", lambda e, s=s, kc=kc, b=b: e.matmul(bank(b), lhsT=mixT[:, kc, s * 128:(s + 1) * 128], rhs=wring[:, slot, kc, :], start=(kc == 0), stop=(kc == 7)),
                             ["mixT", f"w{slot}"], [f"ps{b}"])
                    S.op("dve", lambda e, s=s, b=b: e.scalar_tensor_tensor(xtok[:, s, half * 512:(half + 1) * 512], xtok[:, s, half * 512:(half + 1) * 512], float(ALPHA), bank(b), op0=ALU.mult, op1=ALU.add),
                         [f"xtok{s}", f"ps{b}"], [f"xtok{s}"])
            for s in range(NS):
                layer_norm(xtok, x1, sk("xtok"), sk("x1_"), s)
            make_xT(x1, sk("x1_"))

            for p in range(6):
                ncols = 512 if p < 5 else 256
                s_g = wload(w_gu_d, 0, 8, p * 512, ncols)
                s_u = wload(w_gu_d, 0, 8, DFF + p * 512, ncols)
                for j in range(ncols // 128):
                    fc = p * 4 + j
                    b = fm_group(s_g, j, xT, lambda kc: f"xT{kc}")
                    S.op("act", lambda e, b=b, fc=fc: e.activation(out=gtmp[:, fc % 2, :], in_=bank(b), func=AF.Silu), [f"ps{b}"], [f"gtmp{fc % 2}"])
                    b = fm_group(s_u, j, xT, lambda kc: f"xT{kc}")
                    S.op("dve", lambda e, b=b, fc=fc: e.tensor_tensor(hT[:, fc, :], gtmp[:, fc % 2, :], bank(b), op=ALU.mult), [f"gtmp{fc % 2}", f"ps{b}"], [f"hT{fc}"])

            S.dma("sp", "lnrow", lnrow[:, 0, :], rows_d[4:5, :].partition_broadcast(128), writes=["lnrow"])
            S.dma("sp", "lnrow", lnrow[:, 1, :], rows_d[5:6, :].partition_broadcast(128), writes=["lnrow"])
            dbanks = [0, 1, 2, 5]
            for half in range(2):
                for kp in range(3):
                    nk = 8 if kp < 2 else 6
                    slot = wload(w_dn_d, kp * 1024, nk, half * 512, 512)
                    for s in range(NS):
                        for kk in range(nk):
                            fc = kp * 8 + kk
                            S.op("pe", lambda e, s=s, kk=kk, fc=fc: e.matmul(bank(dbanks[s]), lhsT=hT[:, fc, s * 128:(s + 1) * 128], rhs=wring[:, slot, kk, :], start=(fc == 0), stop=(fc == 21)),
                                 [f"hT{fc}", f"w{slot}"], [f"ps{dbanks[s]}"])
                for s in range(NS):
                    S.op("dve", lambda e, s=s: e.scalar_tensor_tensor(x1[:, s, half * 512:(half + 1) * 512], x1[:, s, half * 512:(half + 1) * 512], float(ALPHA), bank(dbanks[s]), op0=ALU.mult, op1=ALU.add),
                         [f"x1_{s}", f"ps{dbanks[s]}"], [f"x1_{s}"])
            for s in range(NS):
                layer_norm(x1, x1, sk("x1_"), sk("x1_"), s)
                S.dma("sp", f"yst{s}", y_d[t0 + s * 128:t0 + (s + 1) * 128, :], x1[:, s, :], reads=[f"x1_{s}"])

        if full:
            for s in range(NS):
                S.final_wait("sp", [f"x1_{s}"])
        else:
            S.dma("sp", "so0", st_d[:, 0:1024], Rh[:].rearrange("p h e -> p (h e)"), reads=["Rh"])
            S.dma("sp", "so1", st_d[:, 1024:2048], Cs[:].rearrange("p h e -> p (h e)"), reads=["Cs"])
            S.dma("sp", "so2", st_d[:, 2048:2052], smc(C_N), reads=["sm_n"])
            S.dma("sp", "so3", st_d[:, 2052:2056], smc(C_M), reads=["sm_m"])
            S.dma("sp", "so4", st_d[:, 2056:2060], smc(C_NEGF), reads=["sm_negf"])
            S.final_wait("sp", ["Rh", "Cs", "sm_n", "sm_m", "sm_negf"])
        build_program.stats = (S.nops, S.nwaits, dict(S.cnt))
    return nc


def _rope_tables(T, seg):
    pos = (seg * T + np.arange(T)).astype(np.float32)
    half = DK // 2
    inv_freq = (10000.0 ** (-np.arange(half, dtype=np.float32) / half)).astype(np.float32)
    ang = (pos[:, None] * inv_freq[None, :]).astype(np.float32)
    cos = np.cos(ang).astype(np.float64)
    sin = np.sin(ang).astype(np.float64)
    gam = 1.0 - 2.0 ** (-5.0 - np.arange(HEADS, dtype=np.float64))
    i = (np.arange(T) % L).astype(np.float64)
    qd = gam[None, :] ** (i[:, None] + 1.0)
    kd = gam[None, :] ** (-(i[:, None] + 1.0)) * DK ** -0.5
    tab = np.concatenate([
        (cos[:, None, :] * qd[:, :, None]).reshape(T, -1), (sin[:, None, :] * qd[:, :, None]).reshape(T, -1),
        (cos[:, None, :] * kd[:, :, None]).reshape(T, -1), (sin[:, None, :] * kd[:, :, None]).reshape(T, -1)], axis=1)
    return np.ascontiguousarray(tab.astype(np.float32))


def _consts(T, core):
    c = np.zeros((128, NCONST), np.float32)
    c[:, 0:128] = np.eye(128, dtype=np.float32)
    m = (np.arange(128)[:, None] <= np.arange(128)[None, :]).astype(np.float32)
    c[:, 128:640] = np.tile(m, (1, 4))
    c[:, 640:768] = 1.0
    gam = 1.0 - 2.0 ** (-5.0 - np.arange(HEADS, dtype=np.float64))
    c[:, 768:772] = (gam ** L).astype(np.float32)[None, :]
    c[:, 772:776] = (gam ** T).astype(np.float32)[None, :]
    c[:, 776 + core] = 1.0
    c[:, 784] = EPS
    c[:, 785] = -0.5 * math.log(DK)
    c[:, 786] = 1.0
    return c


def _layer_params(l, b_if, b_merge, conv_w, conv_b):
    lp = np.zeros((128, NLP), np.float32)
    lp[:, 0:8] = b_if[l][None, :]
    lp[:, 8:40] = conv_w[l].reshape(4, 8, 128).transpose(2, 1, 0).reshape(128, 32)
    lp[:, 40:48] = conv_b[l].reshape(8, 128).T
    lp[:, 48:64] = b_merge[l].reshape(16, 128).T
    return lp


_PROGS = {}


def _prog(T, mode):
    if (T, mode) not in _PROGS:
        _PROGS[(T, mode)] = build_program(T, mode)
    return _PROGS[(T, mode)]


def run_layers(x, w_in, b_if, b_merge, conv_w, conv_b, ret_norm_g, mlstm_norm_g, w_proj_ret,
               w_proj_mlstm, w_out, ln1_g, ln1_b, w_gate_up, w_down, ln2_g, ln2_b, nseg=NSEG, depth=DEPTH):
    f = lambda a: np.ascontiguousarray(np.asarray(a, dtype=np.float32))
    x = f(x)
    B, Sq, _ = x.shape
    T = Sq // nseg
    ncores = B * nseg
    cores = list(range(ncores))
    ropes = [_rope_tables(T, c % nseg) for c in cores]
    consts = []
    for c in cores:
        cc = _consts(T, 0)
        cc[:, 776:784] = 0.0
        if c % nseg > 0:
            cc[:, 776 + (c // nseg) * NSEG + (c % nseg)] = 1.0
        consts.append(cc)
    for l in range(depth):
        lp = _layer_params(l, f(b_if), f(b_merge), f(conv_w), f(conv_b))
        rows = np.stack([f(ret_norm_g)[l], f(mlstm_norm_g)[l], f(ln1_g)[l], f(ln1_b)[l], f(ln2_g)[l], f(ln2_b)[l]], 0)
        xs, xhs = [], []
        for c in cores:
            b, sg = c // nseg, c % nseg
            xs.append(np.ascontiguousarray(x[b, sg * T:(sg + 1) * T]))
            xhs.append(np.ascontiguousarray(x[b, sg * T - 3:sg * T]) if sg > 0 else np.zeros((3, D), np.float32))
        wi = f(w_in[l])
        base = [dict(x=xs[c], xh=xhs[c], rope=ropes[c], consts=consts[c], lp=lp, rows=rows, w_in=wi) for c in cores]
        if nseg > 1:
            resA = run_bass_kernel_spmd(_prog(T, "A"), base, core_ids=cores)
            allst = np.zeros((NCORES, 128, NST), np.float32)
            for c in cores:
                b, sg = c // nseg, c % nseg
                allst[b * NSEG + sg] = resA.results[c]["st"]
        else:
            allst = np.zeros((NCORES, 128, NST), np.float32)
        extra = dict(allst=allst, w_pr=f(w_proj_ret[l]), w_pm=f(w_proj_mlstm[l]), w_out=f(w_out[l]),
                     w_gu=f(w_gate_up[l]), w_dn=f(w_down[l]))
        resB = run_bass_kernel_spmd(_prog(T, "B"), [dict(d, **extra) for d in base], core_ids=cores)
        xn = np.empty_like(x)
        for c in cores:
            b, sg = c // nseg, c % nseg
            xn[b, sg * T:(sg + 1) * T] = resB.results[c]["y"]
        x = xn
    return x


def kernel(**inputs):
    return run_layers(**inputs)
```
